# Optimizing a Trainium2 kernel written in Bass

```python
import math
import jax, jax.numpy as jnp
from jax import lax
import numpy as np

D_MODEL = 1024
BATCH = 1
SEQ = 16384
DEPTH = 4

D_FF = 2816
CONV_K = 4
SSM_EXPAND = 2
SSM_INNER = SSM_EXPAND * D_MODEL
SSM_HEAD_DIM = 64
SSM_HEADS = SSM_INNER // SSM_HEAD_DIM
SSM_GROUPS = 4
SSM_STATE = 128
SSM_CONV_DIM = SSM_INNER + 2 * SSM_GROUPS * SSM_STATE
SSM_CHUNK = 128
DN_HEADS = 8
DN_HEAD_K = 128
DN_HEAD_V = 128
DN_K_DIM = DN_HEADS * DN_HEAD_K
DN_V_DIM = DN_HEADS * DN_HEAD_V
DN_CONV_DIM = 2 * DN_K_DIM + DN_V_DIM
DN_CHUNK = 64
IN_SPLIT_SIZES = (SSM_INNER, SSM_CONV_DIM, SSM_HEADS,
                  DN_CONV_DIM, DN_V_DIM, DN_HEADS, DN_HEADS,
                  D_MODEL, D_MODEL)
IN_DIM = sum(IN_SPLIT_SIZES)
EPS = 1e-6

kernel_name = "hybrid_ssd_gdn_macaron_trunk"


def rms_norm(x, w):
    xf = x.astype(jnp.float32)
    y = xf * lax.rsqrt(jnp.mean(xf * xf, axis=-1, keepdims=True) + EPS)
    return (y * w.astype(jnp.float32)).astype(x.dtype)


def l2_norm(x):
    xf = x.astype(jnp.float32)
    return xf * lax.rsqrt(jnp.sum(xf * xf, axis=-1, keepdims=True) + EPS)


def swiglu(x, w_in, w_out):
    gate, up = jnp.split(x @ w_in, 2, axis=-1)
    return (jax.nn.silu(gate) * up) @ w_out


def causal_depthwise_conv(x, w, bias=None):
    K, C = w.shape
    y = lax.conv_general_dilated(
        x, w[:, None, :].astype(x.dtype), window_strides=(1,), padding=[(K - 1, 0)],
        dimension_numbers=("NWC", "WIO", "NWC"), feature_group_count=C)
    if bias is not None:
        y = y + bias.astype(x.dtype)
    return y


def ssd_chunked(x, dt, a, Bm, Cm):
    b, S, H, P = x.shape
    G, N = Bm.shape[2], Bm.shape[3]
    hg = H // G
    L = SSM_CHUNK
    nc = S // L
    f32 = jnp.float32
    xdt = (x.astype(f32) * dt[..., None]).reshape(b, nc, L, G, hg, P)
    log_a = (dt * a).reshape(b, nc, L, G, hg)
    Bc = Bm.astype(f32).reshape(b, nc, L, G, N)
    Cc = Cm.astype(f32).reshape(b, nc, L, G, N)
    a_cum = jnp.cumsum(log_a, axis=2)
    causal = jnp.tril(jnp.ones((L, L), dtype=bool))[None, None, :, :, None, None]
    seg = a_cum[:, :, :, None] - a_cum[:, :, None, :]
    decay = jnp.where(causal, jnp.exp(jnp.where(causal, seg, 0.0)), 0.0)
    cb = jnp.einsum("bclgn,bcsgn->bclsg", Cc, Bc)
    y_diag = jnp.einsum("bclsg,bclsgh,bcsghp->bclghp", cb, decay, xdt)
    decay_to_end = jnp.exp(a_cum[:, :, -1:] - a_cum)
    chunk_states = jnp.einsum("bclgn,bclgh,bclghp->bcghpn", Bc, decay_to_end, xdt)
    chunk_decay = jnp.exp(a_cum[:, :, -1])

    def step(state, inp):
        cs, cd = inp
        return state * cd[..., None, None] + cs, state

    s0 = jnp.zeros((b, G, hg, P, N), f32)
    _, states_in = lax.scan(step, s0, (jnp.moveaxis(chunk_states, 1, 0), jnp.moveaxis(chunk_decay, 1, 0)))
    states_in = jnp.moveaxis(states_in, 0, 1)
    y_off = jnp.einsum("bclgn,bcghpn,bclgh->bclghp", Cc, states_in, jnp.exp(a_cum))
    return (y_diag + y_off).reshape(b, S, H, P)


def gated_delta_rule_chunked(q, k, v, g, beta):
    b, S, H, dk = q.shape
    dv = v.shape[-1]
    C = DN_CHUNK
    n = S // C
    f32 = jnp.float32

    def chunks(t):
        return jnp.moveaxis(t.astype(f32).reshape(b, n, C, *t.shape[2:]), 2, 3)

    q, k, v, g, beta = (chunks(t) for t in (q, k, v, g, beta))
    G = jnp.cumsum(g, axis=-1)
    incl = jnp.tril(jnp.ones((C, C), dtype=bool))
    strict = jnp.tril(jnp.ones((C, C), dtype=bool), -1)
    diff = G[..., :, None] - G[..., None, :]
    decay = jnp.where(incl, jnp.exp(jnp.where(incl, diff, 0.0)), 0.0)
    kb = k * beta[..., None]
    M = jnp.where(strict, jnp.einsum("bnhid,bnhjd->bnhij", kb, k) * decay, 0.0)
    eye = jnp.eye(C, dtype=f32)
    rhs = jnp.concatenate([v * beta[..., None], kb * jnp.exp(G)[..., None]], axis=-1)
    sol = lax.linalg.triangular_solve(eye + M, rhs, left_side=True, lower=True, unit_diagonal=True)
    u, w = sol[..., :dv], sol[..., dv:]
    a_qk = jnp.einsum("bnhid,bnhjd->bnhij", q, k) * decay
    q_dec = q * jnp.exp(G)[..., None]
    g_last = G[..., -1]
    k_dec = k * jnp.exp(g_last[..., None] - G)[..., None]

    def step(state, inp):
        q_c, a_c, u_c, w_c, k_c, gl = inp
        v_new = u_c - jnp.einsum("bhck,bhkv->bhcv", w_c, state)
        o = jnp.einsum("bhck,bhkv->bhcv", q_c, state) + jnp.einsum("bhij,bhjv->bhiv", a_c, v_new)
        state = state * jnp.exp(gl)[..., None, None] + jnp.einsum("bhck,bhcv->bhkv", k_c, v_new)
        return state, o

    xs = tuple(jnp.moveaxis(t, 1, 0) for t in (q_dec, a_qk, u, w, k_dec, g_last))
    s0 = jnp.zeros((b, H, dk, dv), f32)
    _, o = lax.scan(step, s0, xs)
    o = jnp.moveaxis(jnp.moveaxis(o, 0, 1), 2, 3)
    return o.reshape(b, S, H, dv)


def mamba2_branch(z, xbc, dt_raw, conv_w, conv_b, dt_bias, a_log, d_skip, norm_w):
    b, S, _ = xbc.shape
    f32 = jnp.float32
    xbc = jax.nn.silu(causal_depthwise_conv(xbc, conv_w, conv_b))
    xs, Bm, Cm = jnp.split(xbc, [SSM_INNER, SSM_INNER + SSM_GROUPS * SSM_STATE], axis=-1)
    xs = xs.reshape(b, S, SSM_HEADS, SSM_HEAD_DIM)
    Bm = Bm.reshape(b, S, SSM_GROUPS, SSM_STATE)
    Cm = Cm.reshape(b, S, SSM_GROUPS, SSM_STATE)
    dt = jax.nn.softplus(dt_raw.astype(f32) + dt_bias.astype(f32))
    a = -jnp.exp(a_log.astype(f32))
    y = ssd_chunked(xs, dt, a, Bm, Cm) + xs.astype(f32) * d_skip.astype(f32)[:, None]
    yg = (y.reshape(b, S, SSM_INNER) * jax.nn.silu(z.astype(f32))).reshape(b, S, SSM_GROUPS, -1)
    yg = yg * lax.rsqrt(jnp.mean(yg * yg, axis=-1, keepdims=True) + EPS)
    return (yg.reshape(b, S, SSM_INNER) * norm_w.astype(f32)).astype(z.dtype)


def deltanet_branch(qkv, z, b_raw, a_raw, conv_w, dt_bias, a_log, norm_w):
    b, S, _ = qkv.shape
    f32 = jnp.float32
    qkv = jax.nn.silu(causal_depthwise_conv(qkv, conv_w))
    q, k, v = jnp.split(qkv, [DN_K_DIM, 2 * DN_K_DIM], axis=-1)
    q = l2_norm(q.reshape(b, S, DN_HEADS, DN_HEAD_K)) * (DN_HEAD_K ** -0.5)
    k = l2_norm(k.reshape(b, S, DN_HEADS, DN_HEAD_K))
    v = v.reshape(b, S, DN_HEADS, DN_HEAD_V)
    beta = jax.nn.sigmoid(b_raw.astype(f32))
    g = -jnp.exp(a_log.astype(f32)) * jax.nn.softplus(a_raw.astype(f32) + dt_bias.astype(f32))
    o = gated_delta_rule_chunked(q, k, v, g, beta)
    o = o * lax.rsqrt(jnp.mean(o * o, axis=-1, keepdims=True) + EPS) * norm_w.astype(f32)
    o = o * jax.nn.silu(z.astype(f32).reshape(b, S, DN_HEADS, DN_HEAD_V))
    return o.reshape(b, S, DN_V_DIM).astype(z.dtype)


def setup_inputs(seed: int = 0) -> dict:
    key = jax.random.key(seed)
    k = jax.random.split(key, 23)
    f32 = jnp.float32

    def dense(kk, shape, fan_in):
        return jax.random.normal(kk, shape, f32) * fan_in ** -0.5

    def gain(kk, shape):
        return 1.0 + 0.02 * jax.random.normal(kk, shape, f32)

    def dt_bias(kk, shape):
        u = jax.random.uniform(kk, shape, f32)
        dt = jnp.exp(u * (math.log(0.1) - math.log(1e-3)) + math.log(1e-3))
        return dt + jnp.log(-jnp.expm1(-dt))

    def a_log(kk, shape):
        return jnp.log(jax.random.uniform(kk, shape, f32, 1.0, 16.0))

    L = DEPTH
    return {
        "x": jax.random.normal(k[0], (BATCH, SEQ, D_MODEL), f32),
        "ffn1_norm": gain(k[1], (L, D_MODEL)),
        "ffn1_w_in": dense(k[2], (L, D_MODEL, 2 * D_FF), D_MODEL),
        "ffn1_w_out": dense(k[3], (L, D_FF, D_MODEL), D_FF),
        "mix_norm": gain(k[4], (L, D_MODEL)),
        "w_in": dense(k[5], (L, D_MODEL, IN_DIM), D_MODEL),
        "ssm_conv_w": dense(k[6], (L, CONV_K, SSM_CONV_DIM), CONV_K),
        "ssm_conv_b": 0.02 * jax.random.normal(k[7], (L, SSM_CONV_DIM), f32),
        "ssm_dt_bias": dt_bias(k[8], (L, SSM_HEADS)),
        "ssm_a_log": a_log(k[9], (L, SSM_HEADS)),
        "ssm_d": 1.0 + 0.1 * jax.random.normal(k[10], (L, SSM_HEADS), f32),
        "ssm_norm": gain(k[11], (L, SSM_INNER)),
        "ssm_w_branch": dense(k[12], (L, SSM_INNER, D_MODEL), SSM_INNER),
        "dn_conv_w": dense(k[13], (L, CONV_K, DN_CONV_DIM), CONV_K),
        "dn_dt_bias": dt_bias(k[14], (L, DN_HEADS)),
        "dn_a_log": a_log(k[15], (L, DN_HEADS)),
        "dn_norm": gain(k[16], (L, DN_HEAD_V)),
        "dn_w_branch": dense(k[17], (L, DN_V_DIM, D_MODEL), DN_V_DIM),
        "w_out": dense(k[18], (L, D_MODEL, D_MODEL), D_MODEL),
        "ffn2_norm": gain(k[19], (L, D_MODEL)),
        "ffn2_w_in": dense(k[20], (L, D_MODEL, 2 * D_FF), D_MODEL),
        "ffn2_w_out": dense(k[21], (L, D_FF, D_MODEL), D_FF),
        "final_norm": gain(k[22], (D_MODEL,)),
    }


def reference(x, ffn1_norm, ffn1_w_in, ffn1_w_out, mix_norm, w_in, ssm_conv_w, ssm_conv_b,
              ssm_dt_bias, ssm_a_log, ssm_d, ssm_norm, ssm_w_branch, dn_conv_w, dn_dt_bias,
              dn_a_log, dn_norm, dn_w_branch, w_out, ffn2_norm, ffn2_w_in, ffn2_w_out, final_norm):
    split_points = [int(p) for p in np.cumsum(IN_SPLIT_SIZES)[:-1]]
    h = x
    for l in range(DEPTH):
        h = h + 0.5 * swiglu(rms_norm(h, ffn1_norm[l]), ffn1_w_in[l], ffn1_w_out[l])
        u = rms_norm(h, mix_norm[l])
        (z_s, xbc, dt_s, qkv, z_d, b_d, a_d, gate_s, gate_d) = jnp.split(u @ w_in[l], split_points, axis=-1)
        y_s = mamba2_branch(z_s, xbc, dt_s, ssm_conv_w[l], ssm_conv_b[l], ssm_dt_bias[l],
                            ssm_a_log[l], ssm_d[l], ssm_norm[l])
        y_d = deltanet_branch(qkv, z_d, b_d, a_d, dn_conv_w[l], dn_dt_bias[l],
                              dn_a_log[l], dn_norm[l])
        merged = (jax.nn.sigmoid(gate_s) * (y_s @ ssm_w_branch[l])
                  + jax.nn.sigmoid(gate_d) * (y_d @ dn_w_branch[l]))
        h = h + merged @ w_out[l]
        h = h + 0.5 * swiglu(rms_norm(h, ffn2_norm[l]), ffn2_w_in[l], ffn2_w_out[l])
    return rms_norm(h, final_norm)
```

```python
import contextlib
import numpy as np
import ml_dtypes
import concourse.bass as bass
import concourse.mybir as mybir
from concourse.bass_utils import run_bass_kernel_spmd

F32 = mybir.dt.float32
BF16 = mybir.dt.bfloat16
AF = mybir.ActivationFunctionType
OP = mybir.AluOpType
AX = mybir.AxisListType

D_MODEL = 1024
SEQ = 16384
DEPTH = 4
D_FF = 2816
NCORE = 8
TPC = SEQ // NCORE
EPS = 1e-6

ENGS = ("pe", "act", "dve", "pool", "sp")
N_DMA_SLOTS = 24
SAME_ENGINE_SYNC = True


class Prog:
    def __init__(self, nc):
        self.nc = nc
        self.ops = []
        self.dma_rr = 0

    def op(self, eng, fn, reads=(), writes=()):
        self.ops.append(dict(eng=eng, fn=fn, reads=tuple(reads), writes=tuple(writes), dma=None))

    def dma(self, q, out, in_, reads=(), writes=(), **kw):
        slot = self.dma_rr % N_DMA_SLOTS
        self.dma_rr += 1
        self.ops.append(dict(eng=q, fn=lambda e: e.dma_start(out=out, in_=in_, **kw),
                             reads=tuple(reads), writes=tuple(writes), dma=slot))

    def emit(self):
        nc = self.nc
        ops = self.ops
        idx_in_eng = {e: 0 for e in ENGS}
        slot_n = [0] * N_DMA_SLOTS
        last_w = {}
        reads = {}
        for o in ops:
            e = o["eng"]
            deps = {}

            def add(tok):
                if tok is None:
                    return
                sk, v = tok
                if deps.get(sk, -1) < v:
                    deps[sk] = v
            for k in o["reads"]:
                add(last_w.get(k))
            for k in o["writes"]:
                add(last_w.get(k))
                for sk, v in reads.get(k, {}).items():
                    add((sk, v))
            if o["dma"] is not None:
                s = o["dma"]
                if slot_n[s] > 0:
                    add((("dma", s), slot_n[s]))
                slot_n[s] += 1
                tok = (("dma", s), slot_n[s])
            else:
                idx_in_eng[e] += 1
                tok = (("eng", e), idx_in_eng[e])
            o["tok"] = tok
            o["deps"] = deps
            for k in o["reads"]:
                d = reads.setdefault(k, {})
                if d.get(tok[0], -1) < tok[1]:
                    d[tok[0]] = tok[1]
            for k in o["writes"]:
                last_w[k] = tok
                reads[k] = {}
        waited = {e: {} for e in ENGS}
        marked = set()
        for o in ops:
            e = o["eng"]
            need = []
            for sk, v in o["deps"].items():
                if sk[0] == "eng" and sk[1] == e:
                    if e == "pe" or not SAME_ENGINE_SYNC:
                        continue
                if waited[e].get(sk, 0) >= v:
                    continue
                waited[e][sk] = v
                need.append((sk, v))
                if sk[0] == "eng":
                    marked.add((sk[1], v))
            o["need"] = need
        cnt = {e: 0 for e in ENGS}
        semval = {}
        k = {e: 0 for e in ENGS}
        for o in ops:
            if o["dma"] is not None:
                continue
            e = o["eng"]
            k[e] += 1
            if (e, k[e]) in marked:
                cnt[e] += 1
                semval[(e, k[e])] = cnt[e]
                o["inc"] = True
            else:
                o["inc"] = False
        self.stats = dict(n_ops=len(ops), marked=len(marked), per_eng=dict(k))
        with contextlib.ExitStack() as st:
            esem = {e: st.enter_context(nc.semaphore("s_" + e)) for e in ENGS}
            dsem = [st.enter_context(nc.semaphore("d_%d" % i)) for i in range(N_DMA_SLOTS)]
            block = st.enter_context(nc.Block())

            def stream(e, eng):
                for o in ops:
                    if o["eng"] != e:
                        continue
                    for sk, v in o["need"]:
                        if sk[0] == "eng":
                            eng.wait_ge(esem[sk[1]], semval[(sk[1], v)])
                        else:
                            eng.wait_ge(dsem[sk[1]], 16 * v)
                    ins = o["fn"](eng)
                    if o["dma"] is not None:
                        ins.then_inc(dsem[o["dma"]], 16)
                    elif o["inc"]:
                        ins.then_inc(esem[e], 1)
                if e == "sp":
                    for s in range(N_DMA_SLOTS):
                        if slot_n[s] > 0:
                            eng.wait_ge(dsem[s], 16 * slot_n[s])

            @block.tensor
            def _(eng):
                stream("pe", eng)

            @block.scalar
            def _(eng):
                stream("act", eng)

            @block.vector
            def _(eng):
                stream("dve", eng)

            @block.gpsimd
            def _(eng):
                stream("pool", eng)

            @block.sync
            def _(eng):
                stream("sp", eng)


def fap(base, dims):
    return bass.AP(base.tensor, base.offset, [list(base.ap[0])] + [list(d) for d in dims])


class Ctx:
    def __init__(self, nc, st):
        self.nc = nc
        self.st = st
        self.P = Prog(nc)
        self.ps_ring = []
        self.ps_i = 0

    def sb(self, name, shape, dt):
        return self.st.enter_context(self.nc.sbuf_tensor(name, shape, dt))

    def ps(self, name, shape, dt):
        return self.st.enter_context(self.nc.psum_tensor(name, shape, dt))

    def make_ps_ring(self, n):
        self.ps_ring = [(self.ps("psr%d" % i, [128, 512], F32), "psr%d" % i) for i in range(n)]

    def next_ps(self):
        t = self.ps_ring[self.ps_i % len(self.ps_ring)]
        self.ps_i += 1
        return t

    def mm(self, out, lhsT, rhs, start, stop, r, w):
        self.P.op("pe", lambda e: e.matmul(out, lhsT=lhsT, rhs=rhs, start=start, stop=stop), reads=r, writes=w)

    def tr(self, out, in_, ident, r, w):
        self.P.op("pe", lambda e: e.transpose(out, in_, ident), reads=r, writes=w)

    def act(self, out, in_, func, r, w, bias=None, scale=None):
        kw = {}
        if bias is not None:
            kw["bias"] = bias
        if scale is not None:
            kw["scale"] = scale
        self.P.op("act", lambda e: e.activation(out=out, in_=in_, func=func, **kw), reads=r, writes=w)

    def tt(self, eng, out, in0, in1, op, r, w):
        self.P.op(eng, lambda e: e.tensor_tensor(out=out, in0=in0, in1=in1, op=op), reads=r, writes=w)

    def ts(self, eng, out, in0, s1, op0, r, w, s2=None, op1=None):
        if op1 is None:
            self.P.op(eng, lambda e: e.tensor_scalar(out=out, in0=in0, scalar1=s1, scalar2=None, op0=op0), reads=r, writes=w)
        else:
            self.P.op(eng, lambda e: e.tensor_scalar(out=out, in0=in0, scalar1=s1, scalar2=s2, op0=op0, op1=op1), reads=r, writes=w)

    def stt(self, eng, out, in0, scalar, in1, op0, op1, r, w):
        eng = "dve"
        self.P.op(eng, lambda e: e.scalar_tensor_tensor(out=out, in0=in0, scalar=scalar, in1=in1, op0=op0, op1=op1),
                  reads=r, writes=w)

    def cp(self, eng, out, in_, r, w):
        if eng == "act":
            self.P.op("act", lambda e: e.copy(out=out, in_=in_), reads=r, writes=w)
        else:
            self.P.op(eng, lambda e: e.tensor_copy(out=out, in_=in_), reads=r, writes=w)

    def memset(self, eng, ap, val, w):
        self.P.op(eng, lambda e: e.memset(ap, val), writes=w)

    def recip(self, out, in_, r, w):
        self.P.op("dve", lambda e: e.reciprocal(out=out, in_=in_), reads=r, writes=w)


def _consts():
    c = np.zeros((128, 5, 128), np.float32)
    r = np.arange(128)[:, None]
    col = np.arange(128)[None, :]
    c[:, 0, :] = (r == col)
    c[:, 1, :] = (col >= r)
    c[:, 2, :] = (col > r)
    c[:, 3, :] = (r > col)
    c[:, 4, :] = 1.0
    return c


NWB = 902
PV_CW, PV_CB, PV_DTB, PV_ALOG, PV_D, PV_DDTB, PV_DALOG, PV_DNW, PV_N = 0, 28, 35, 39, 43, 47, 48, 49, 177


def build_B(ntok):
    nblk = ntok // 512
    nc = bass.Bass("TRN2", target_bir_lowering=False)
    uT = nc.dram_tensor("uT", [1024, ntok], BF16, kind="ExternalInput").ap()
    wB = nc.dram_tensor("wB", [1024, NWB], F32, kind="ExternalInput").ap()
    cst_d = nc.dram_tensor("cst", [128, 5, 128], F32, kind="ExternalInput").ap()
    pv_d = nc.dram_tensor("pv", [128, PV_N], F32, kind="ExternalInput").ap()
    ys_d = nc.dram_tensor("ys", [ntok, 256], F32, kind="ExternalOutput").ap()
    yd_d = nc.dram_tensor("yd", [ntok, 128], F32, kind="ExternalOutput").ap()
    with contextlib.ExitStack() as st:
        C = Ctx(nc, st)
        P = C.P
        sb = C.sb
        wb = sb("wb", [128, 8, NWB], BF16)
        cst = sb("cst_sb", [128, 5, 128], F32)
        identb = sb("identb", [128, 128], BF16)
        pv = sb("pv_sb", [128, PV_N], F32)
        halo = sb("halo", [128, 7, 3], F32)
        ST32 = sb("ST32", [128, 256], F32)
        STb = sb("STb", [128, 256], BF16)
        S32 = sb("S32", [128, 128], F32)
        Sb = sb("Sb", [128, 128], BF16)
        negA = sb("negA", [128, 4], F32)
        negAd = sb("negAd", [128, 1], F32)
        I_ = cst[:, 0, :]
        UTI = cst[:, 1, :]
        ONES = cst[:, 4, :]
        I64 = cst[0:64, 0, 0:64]
        UTI64 = cst[0:64, 1, 0:64]
        UTS64 = cst[0:64, 2, 0:64]
        LTS64 = cst[0:64, 3, 0:64]
        ONES64 = cst[0:64, 4, :]
        ub = [sb("ub%d" % i, [128, 8, 512], BF16) for i in range(2)]
        pre = [sb("pre%d" % i, [128, 515], F32) for i in range(2)]
        acc = [sb("acc%d" % i, [128, 512], F32) for i in range(2)]
        xTb = [sb("xTb%d" % i, [128, 2, 512], BF16) for i in range(2)]
        BTb = [sb("BTb%d" % i, [128, 512], BF16) for i in range(2)]
        CTb = [sb("CTb%d" % i, [128, 512], BF16) for i in range(2)]
        q32 = [sb("q32%d" % i, [128, 512], F32) for i in range(2)]
        k32 = [sb("k32%d" % i, [128, 512], F32) for i in range(2)]
        vTb = [sb("vTb%d" % i, [128, 512], BF16) for i in range(2)]
        sq = sb("sq", [128, 512], F32)
        rn = sb("rn", [128, 512], F32)
        qnT = [sb("qnT%d" % i, [128, 512], BF16) for i in range(2)]
        knT = [sb("knT%d" % i, [128, 512], BF16) for i in range(2)]
        dtr = sb("dtr", [128, 16], F32)
        dt = sb("dt", [128, 4, 4], F32)
        la = sb("la", [128, 4, 4], F32)
        beta = sb("beta", [64, 8], F32)
        gg = sb("gg", [64, 8], F32)
        xtok = sb("xtok", [128, 256], F32)
        xdt = sb("xdt", [128, 256], BF16)
        Btok = sb("Btok", [128, 128], BF16)
        acum = sb("acum", [128, 4], F32)
        TLA = sb("TLA", [128, 4, 128], F32)
        seg = sb("seg", [128, 4, 128], F32)
        dec = sb("dec", [128, 4, 128], F32)
        dte = sb("dte", [128, 4], F32)
        cd = sb("cd", [128, 4], F32)
        eac = sb("eac", [128, 4], F32)
        cbm = sb("cbm", [128, 128], F32)
        LT = sb("LT", [128, 4, 128], BF16)
        ydt = sb("ydt", [128, 256], F32)
        yot = sb("yot", [128, 256], F32)
        xdte = sb("xdte", [128, 256], BF16)
        stt_ = sb("stt_", [128, 256], F32)
        yout = [sb("yout%d" % i, [128, 4, 256], F32) for i in range(2)]
        Gcol = sb("Gcol", [64, 8], F32)
        TG = sb("TG", [64, 8, 64], F32)
        Grow = sb("Grow", [128, 8, 64], F32)
        eGrow = sb("eGrow", [128, 8, 64], F32)
        qdec = sb("qdec", [128, 512], BF16)
        egl = sb("egl", [128, 8], F32)
        D1 = sb("D1", [64, 8, 64], F32)
        eA = sb("eA", [64, 8, 64], F32)
        eB = sb("eB", [64, 8, 64], F32)
        DB = sb("DB", [64, 8, 64], F32)
        WA = sb("WA", [64, 8, 64], F32)
        AQ = sb("AQ", [64, 8, 64], F32)
        WBm = sb("WBm", [64, 8, 64], F32)
        eGc = sb("eGc", [64, 8], F32)
        bk = sb("bk", [64, 8], F32)
        kd = sb("kd", [64, 8], F32)
        Pa = [sb("Pa%d" % i, [64, 8, 64], F32) for i in range(2)]
        PTa = [sb("PTa%d" % i, [64, 8, 64], F32) for i in range(2)]
        RT = sb("RT", [64, 8, 64], F32)
        RTb = sb("RTb", [64, 8, 64], BF16)
        aqT = sb("aqT", [64, 8, 64], BF16)
        rhs_tok = sb("rhs_tok", [64, 8, 256], BF16)
        kdec = sb("kdec", [64, 8, 128], BF16)
        u_sb = sb("u_sb", [64, 8, 128], F32)
        wT = sb("wT", [128, 8, 64], BF16)
        vnew = sb("vnew", [64, 128], BF16)
        o_sb = sb("o_sb", [64, 8, 128], F32)
        osq = sb("osq", [64, 8, 128], F32)
        oss = sb("oss", [64, 8], F32)
        ydo = [sb("ydo%d" % i, [64, 8, 128], F32) for i in range(2)]
        C.make_ps_ring(6)
        pT = [(C.ps("pT%d" % i, [128, 1024], BF16), "pT%d" % i) for i in range(2)]

        P.dma("pool", wb[:], wB.rearrange("(k p) n -> p k n", p=128), writes=["wb"])
        P.dma("sp", cst[:], cst_d, writes=["cst"])
        P.dma("sp", pv[:], pv_d, writes=["pv"])
        C.cp("dve", identb[:], I_, ["cst"], ["identb"])
        C.memset("pool", halo[:], 0.0, ["halo"])
        C.memset("pool", ST32[:], 0.0, ["ST32"])
        C.memset("pool", STb[:], 0.0, ["STb"])
        C.memset("pool", S32[:], 0.0, ["S32"])
        C.memset("pool", Sb[:], 0.0, ["Sb"])
        C.act(negA[:], pv[:, PV_ALOG:PV_ALOG + 4], AF.Exp, ["pv"], ["negA"])
        C.ts("dve", negA[:], negA[:], -1.0, OP.mult, ["negA"], ["negA"])
        C.act(negAd[:], pv[:, PV_DALOG:PV_DALOG + 1], AF.Exp, ["pv"], ["negAd"])
        C.ts("dve", negAd[:], negAd[:], -1.0, OP.mult, ["negAd"], ["negAd"])

        for blk in range(nblk):
            pb = blk % 2
            t0 = blk * 512
            kub = "ub%d" % pb
            P.dma("sp", ub[pb][:], uT[:, t0:t0 + 512].rearrange("(k p) t -> p k t", p=128), writes=[kub])
            dests = [(xTb[pb][:, 0, :], "xTb%d" % pb), (xTb[pb][:, 1, :], "xTb%d" % pb), (BTb[pb][:], "BTb%d" % pb),
                     (CTb[pb][:], "CTb%d" % pb), (q32[pb][:], "q32%d" % pb), (k32[pb][:], "k32%d" % pb),
                     (vTb[pb][:], "vTb%d" % pb)]
            for ch in range(7):
                pc = ch % 2
                kpre, kacc = "pre%d" % pc, "acc%d" % pc
                pst, pk = C.next_ps()
                for k in range(8):
                    C.mm(pst[:], wb[:, k, ch * 128:(ch + 1) * 128], ub[pb][:, k, :], k == 0, k == 7, ["wb", kub], [pk])
                C.cp("pool", pre[pc][:, 0:3], halo[:, ch, :], ["halo"], [kpre])
                C.cp("act", pre[pc][:, 3:515], pst[:], [pk], [kpre])
                C.cp("pool", halo[:, ch, :], pre[pc][:, 512:515], [kpre], ["halo"])
                cw = lambda j: pv[:, PV_CW + ch * 4 + j:PV_CW + ch * 4 + j + 1]
                ce = "dve" if ch % 2 == 0 else "pool"
                C.ts(ce, acc[pc][:], pre[pc][:, 0:512], cw(0), OP.mult, [kpre, "pv"], [kacc])
                for j in range(1, 4):
                    C.stt(ce, acc[pc][:], pre[pc][:, j:j + 512], cw(j), acc[pc][:], OP.mult, OP.add, [kpre, "pv", kacc], [kacc])
                d_ap, d_k = dests[ch]
                C.act(d_ap, acc[pc][:], AF.Silu, [kacc, "pv"], [d_k], bias=pv[:, PV_CB + ch:PV_CB + ch + 1])
            for (src, ks, dst, kdst, scl) in ((q32[pb], "q32%d" % pb, qnT[pb], "qnT%d" % pb, 128.0 ** -0.5),
                                              (k32[pb], "k32%d" % pb, knT[pb], "knT%d" % pb, 1.0)):
                C.tt("pool", sq[:], src[:], src[:], OP.mult, [ks], ["sq"])
                pst, pk = C.next_ps()
                C.mm(pst[:], ONES, sq[:], True, True, ["cst", "sq"], [pk])
                C.act(rn[:], pst[:], AF.Sqrt, [pk], ["rn"], bias=EPS, scale=1.0)
                C.recip(rn[:], rn[:], ["rn"], ["rn"])
                C.stt("dve", dst[:], src[:], scl, rn[:], OP.mult, OP.mult, [ks, "rn"], [kdst])
            psm, pkm = C.next_ps()
            for sc in range(4):
                for k in range(8):
                    C.mm(psm[:, sc * 4:(sc + 1) * 4], ub[pb][:, k, sc * 128:(sc + 1) * 128], wb[:, k, 896:900],
                         k == 0, k == 7, ["wb", kub], [pkm])
            for c in range(8):
                for k in range(8):
                    C.mm(psm[0:64, 16 + c * 2:18 + c * 2], ub[pb][:, k, c * 64:(c + 1) * 64], wb[:, k, 900:902],
                         k == 0, k == 7, ["wb", kub], [pkm])
            C.tt("dve", dtr[:].rearrange("p (s h) -> p s h", h=4), psm[:, 0:16].rearrange("p (s h) -> p s h", h=4),
                 fap(pv[:, PV_DTB:PV_DTB + 4], [[0, 4], [1, 4]]), OP.add, [pkm, "pv"], ["dtr"])
            C.act(dtr[:], dtr[:], AF.Exp, ["dtr"], ["dtr"])
            C.act(dt[:].rearrange("p s h -> p (s h)"), dtr[:], AF.Ln, ["dtr"], ["dt"], bias=1.0)
            C.tt("dve", la[:], dt[:], fap(negA[:], [[0, 4], [1, 4]]), OP.mult, ["dt", "negA"], ["la"])
            braw = fap(psm[0:64, 16:17], [[2, 8]])
            araw = fap(psm[0:64, 17:18], [[2, 8]])
            C.act(beta[:], braw, AF.Sigmoid, [pkm], ["beta"])
            C.act(gg[:], araw, AF.Exp, [pkm, "pv"], ["gg"], bias=pv[0:64, PV_DDTB:PV_DDTB + 1])
            C.act(gg[:], gg[:], AF.Ln, ["gg"], ["gg"], bias=1.0)
            C.ts("dve", gg[:], gg[:], negAd[0:64, 0:1], OP.mult, ["gg", "negAd"], ["gg"])

            kyo = "yout%d" % pb
            for sc in range(4):
                tk = slice(sc * 128, (sc + 1) * 128)
                ptt, pkt = pT[sc % 2]
                C.tr(ptt[:, 0:128], xTb[pb][:, 0, tk], identb[:], ["xTb%d" % pb, "identb"], [pkt])
                C.tr(ptt[:, 128:256], xTb[pb][:, 1, tk], identb[:], ["xTb%d" % pb, "identb"], [pkt])
                C.tr(ptt[:, 256:384], BTb[pb][:, tk], identb[:], ["BTb%d" % pb, "identb"], [pkt])
                C.cp("act", xtok[:], ptt[:, 0:256], [pkt], ["xtok"])
                C.tt("dve", xdt[:].rearrange("p (h q) -> p h q", q=64), ptt[:, 0:256].rearrange("p (h q) -> p h q", q=64),
                     fap(dt[:, sc, :], [[1, 4], [0, 64]]), OP.mult, [pkt, "dt"], ["xdt"])
                C.cp("act", Btok[:], ptt[:, 256:384], [pkt], ["Btok"])
                ps1, pk1 = C.next_ps()
                C.mm(ps1[:, 0:4], UTI, la[:, sc, :], True, True, ["cst", "la"], [pk1])
                C.cp("act", acum[:], ps1[:, 0:4], [pk1], ["acum"])
                C.tt("dve", TLA[:], fap(UTI, [[0, 4], [1, 128]]), fap(la[:, sc, :], [[1, 4], [0, 128]]), OP.mult,
                     ["cst", "la"], ["TLA"])
                ps2, pk2 = C.next_ps()
                C.mm(ps2[:], ONES, TLA[:].rearrange("p h l -> p (h l)"), True, True, ["cst", "TLA"], [pk2])
                ps2v = ps2[:].rearrange("p (h l) -> p h l", l=128)
                C.tt("dve", seg[:], ps2v, fap(acum[:], [[1, 4], [0, 128]]), OP.subtract, [pk2, "acum"], ["seg"])
                C.ts("pool", seg[:], seg[:], 0.0, OP.min, ["seg"], ["seg"])
                C.act(dec[:], seg[:], AF.Exp, ["seg"], ["dec"])
                alast = fap(ps2[:, 127:128], [[128, 4]])
                C.tt("dve", dte[:], alast, acum[:], OP.subtract, [pk2, "acum"], ["dte"])
                C.act(dte[:], dte[:], AF.Exp, ["dte"], ["dte"])
                C.act(cd[:], alast, AF.Exp, [pk2], ["cd"])
                C.act(eac[:], acum[:], AF.Exp, ["acum"], ["eac"])
                ps3, pk3 = C.next_ps()
                C.mm(ps3[:, 0:128], BTb[pb][:, tk], CTb[pb][:, tk], True, True, ["BTb%d" % pb, "CTb%d" % pb], [pk3])
                C.tt("dve", cbm[:], ps3[:, 0:128], UTI, OP.mult, [pk3, "cst"], ["cbm"])
                C.tt("pool", LT[:], dec[:], fap(cbm[:], [[0, 4], [1, 128]]), OP.mult, ["dec", "cbm"], ["LT"])
                ps4, pk4 = C.next_ps()
                for h in range(4):
                    C.mm(ps4[:, h * 64:(h + 1) * 64], LT[:, h, :], xdt[:, h * 64:(h + 1) * 64], True, True, ["LT", "xdt"], [pk4])
                ps5, pk5 = C.next_ps()
                C.mm(ps5[:, 0:256], CTb[pb][:, tk], STb[:], True, True, ["CTb%d" % pb, "STb"], [pk5])
                h4 = lambda ap: ap.rearrange("p (h q) -> p h q", q=64)
                C.tt("pool", h4(ydt[:]), h4(xtok[:]), fap(pv[:, PV_D:PV_D + 4], [[1, 4], [0, 64]]), OP.mult, ["xtok", "pv"], ["ydt"])
                C.tt("dve", ydt[:], ydt[:], ps4[:, 0:256], OP.add, ["ydt", pk4], ["ydt"])
                C.tt("dve", h4(yot[:]), h4(ps5[:, 0:256]), fap(eac[:], [[1, 4], [0, 64]]), OP.mult, [pk5, "eac"], ["yot"])
                C.tt("pool", yout[pb][:, sc, :], yot[:], ydt[:], OP.add, ["yot", "ydt"], [kyo])
                C.tt("pool", h4(xdte[:]), h4(xdt[:]), fap(dte[:], [[1, 4], [0, 64]]), OP.mult, ["xdt", "dte"], ["xdte"])
                ps6, pk6 = C.next_ps()
                C.mm(ps6[:, 0:256], Btok[:], xdte[:], True, True, ["Btok", "xdte"], [pk6])
                C.tt("pool", h4(stt_[:]), h4(ST32[:]), fap(cd[:], [[1, 4], [0, 64]]), OP.mult, ["ST32", "cd"], ["stt_"])
                C.tt("dve", ST32[:], stt_[:], ps6[:, 0:256], OP.add, ["stt_", pk6], ["ST32"])
                C.cp("act", STb[:], ST32[:], ["ST32"], ["STb"])
            P.dma("sp", ys_d[t0:t0 + 512, :].rearrange("(s p) c -> p s c", p=128), yout[pb][:], reads=[kyo])

            kkn, kqn = "knT%d" % pb, "qnT%d" % pb
            pkt_t, pkt_k = pT[0]
            pvt_t, pvt_k = pT[1]
            for c in range(8):
                C.tr(pkt_t[0:64, c * 128:(c + 1) * 128], knT[pb][:, c * 64:(c + 1) * 64], identb[:], [kkn, "identb"], [pkt_k])
            for c in range(8):
                C.tr(pvt_t[0:64, c * 128:(c + 1) * 128], vTb[pb][:, c * 64:(c + 1) * 64], identb[:], ["vTb%d" % pb, "identb"], [pvt_k])
            ktok_ps = pkt_t[0:64, :].rearrange("p (c d) -> p c d", d=128)
            vtok_ps = pvt_t[0:64, :].rearrange("p (c d) -> p c d", d=128)
            ps1, pk1 = C.next_ps()
            C.mm(ps1[0:64, 0:8], UTI64, gg[:], True, True, ["cst", "gg"], [pk1])
            C.cp("act", Gcol[:], ps1[0:64, 0:8], [pk1], ["Gcol"])
            C.tt("dve", TG[:], fap(UTI64, [[0, 8], [1, 64]]), fap(gg[:], [[1, 8], [0, 64]]), OP.mult, ["cst", "gg"], ["TG"])
            ps2, pk2 = C.next_ps()
            C.mm(ps2[:], ONES64, TG[:].rearrange("p c i -> p (c i)"), True, True, ["cst", "TG"], [pk2])
            C.cp("act", Grow[:].rearrange("p c i -> p (c i)"), ps2[:], [pk2], ["Grow"])
            C.act(eGrow[:].rearrange("p c i -> p (c i)"), ps2[:], AF.Exp, [pk2], ["eGrow"])
            C.tt("dve", qdec[:], qnT[pb][:], eGrow[:].rearrange("p c i -> p (c i)"), OP.mult, [kqn, "eGrow"], ["qdec"])
            C.cp("pool", egl[:], fap(eGrow[:, 0, 63:64], [[64, 8]]), ["eGrow"], ["egl"])
            C.tt("dve", D1[:], Grow[0:64, :, :], fap(Gcol[:], [[1, 8], [0, 64]]), OP.subtract, ["Grow", "Gcol"], ["D1"])
            C.ts("pool", eA[:], D1[:], 0.0, OP.min, ["D1"], ["eA"])
            C.act(eA[:], eA[:], AF.Exp, ["eA"], ["eA"])
            C.ts("pool", eB[:], D1[:], -1.0, OP.mult, ["D1"], ["eB"], s2=0.0, op1=OP.min)
            C.act(eB[:], eB[:], AF.Exp, ["eB"], ["eB"])
            C.tt("dve", DB[:], fap(I64, [[0, 8], [1, 64]]), fap(beta[:], [[1, 8], [0, 64]]), OP.mult, ["cst", "beta"], ["DB"])
            ps3, pk3 = C.next_ps()
            C.mm(ps3[0:64, :], ONES64[:, 0:64], DB[:].rearrange("p c i -> p (c i)"), True, True, ["cst", "DB"], [pk3])
            C.tt("pool", WA[:], eA[:], fap(UTS64, [[0, 8], [1, 64]]), OP.mult, ["eA", "cst"], ["WA"])
            C.tt("dve", WA[:].rearrange("p c i -> p (c i)"), WA[:].rearrange("p c i -> p (c i)"), ps3[0:64, :], OP.mult, ["WA", pk3], ["WA"])
            C.tt("pool", AQ[:], eA[:], fap(UTI64, [[0, 8], [1, 64]]), OP.mult, ["eA", "cst"], ["AQ"])
            C.tt("pool", WBm[:], eB[:], fap(LTS64, [[0, 8], [1, 64]]), OP.mult, ["eB", "cst"], ["WBm"])
            C.tt("pool", WBm[:], WBm[:], fap(beta[:], [[1, 8], [0, 64]]), OP.mult, ["WBm", "beta"], ["WBm"])
            C.act(eGc[:], Gcol[:], AF.Exp, ["Gcol"], ["eGc"])
            C.tt("dve", bk[:], beta[:], eGc[:], OP.mult, ["beta", "eGc"], ["bk"])
            C.tt("dve", kd[:], fap(Grow[0:64, 0, 63:64], [[64, 8]]), Gcol[:], OP.subtract, ["Grow", "Gcol"], ["kd"])
            C.act(kd[:], kd[:], AF.Exp, ["kd"], ["kd"])
            ps4, pk4 = C.next_ps()
            ps5, pk5 = C.next_ps()
            for c in range(8):
                cs = slice(c * 64, (c + 1) * 64)
                C.mm(ps4[0:64, cs], knT[pb][:, cs], knT[pb][:, cs], True, True, [kkn], [pk4])
            for c in range(8):
                cs = slice(c * 64, (c + 1) * 64)
                C.mm(ps5[0:64, cs], knT[pb][:, cs], qnT[pb][:, cs], True, True, [kkn, kqn], [pk5])
            f2 = lambda t: t[:].rearrange("p c i -> p (c i)")
            C.tt("dve", f2(Pa[0]), ps4[0:64, :], f2(WA), OP.mult, [pk4, "WA"], ["Pa0"])
            C.tt("dve", f2(PTa[0]), ps4[0:64, :], f2(WBm), OP.mult, [pk4, "WBm"], ["PTa0"])
            C.tt("dve", f2(aqT), ps5[0:64, :], f2(AQ), OP.mult, [pk5, "AQ"], ["aqT"])
            C.stt("pool", RT[:], Pa[0][:], -1.0, fap(I64, [[0, 8], [1, 64]]), OP.mult, OP.add, ["Pa0", "cst"], ["RT"])
            cur = 0
            for s in range(1, 6):
                nxt = 1 - cur
                kP, kPT, kPn, kPTn = "Pa%d" % cur, "PTa%d" % cur, "Pa%d" % nxt, "PTa%d" % nxt
                if s < 5:
                    psA, pkA = C.next_ps()
                    for c in range(8):
                        C.mm(psA[0:64, c * 64:(c + 1) * 64], PTa[cur][:, c, :], Pa[cur][:, c, :], True, True, [kP, kPT], [pkA])
                psB, pkB = C.next_ps()
                for c in range(8):
                    C.mm(psB[0:64, c * 64:(c + 1) * 64], Pa[cur][:, c, :], PTa[cur][:, c, :], True, True, [kP, kPT], [pkB])
                if s < 5:
                    C.cp("act", f2(Pa[nxt]), psA[0:64, :], [pkA], [kPn])
                C.cp("dve", f2(PTa[nxt]), psB[0:64, :], [pkB], [kPTn])
                psC, pkC = C.next_ps()
                for c in range(8):
                    C.mm(psC[0:64, c * 64:(c + 1) * 64], PTa[nxt][:, c, :], RT[:, c, :], True, True, [kPTn, "RT"], [pkC])
                C.tt("dve", f2(RT), f2(RT), psC[0:64, :], OP.add, ["RT", pkC], ["RT"])
                cur = nxt
            C.cp("act", RTb[:], RT[:], ["RT"], ["RTb"])
            C.tt("dve", rhs_tok[:, :, 0:128], vtok_ps, fap(beta[:], [[1, 8], [0, 128]]), OP.mult, [pvt_k, "beta"], ["rhs_tok"])
            C.tt("dve", rhs_tok[:, :, 128:256], ktok_ps, fap(bk[:], [[1, 8], [0, 128]]), OP.mult, [pkt_k, "bk"], ["rhs_tok"])
            C.tt("dve", kdec[:], ktok_ps, fap(kd[:], [[1, 8], [0, 128]]), OP.mult, [pkt_k, "kd"], ["kdec"])
            for half in range(2):
                psu, pku = C.next_ps()
                for cc in range(4):
                    c = half * 4 + cc
                    C.mm(psu[0:64, cc * 128:(cc + 1) * 128], RTb[:, c, :], rhs_tok[:, c, 0:128], True, True, ["RTb", "rhs_tok"], [pku])
                C.cp("act", u_sb[:, half * 4:(half + 1) * 4, :].rearrange("p c v -> p (c v)"), psu[0:64, :], [pku], ["u_sb"])
            psw, pkw = C.next_ps()
            for c in range(8):
                C.mm(psw[:, c * 64:(c + 1) * 64], rhs_tok[:, c, 128:256], RTb[:, c, :], True, True, ["RTb", "rhs_tok"], [pkw])
            C.cp("act", wT[:].rearrange("p c i -> p (c i)"), psw[:], [pkw], ["wT"])
            for c in range(8):
                cs = slice(c * 64, (c + 1) * 64)
                psa, pka = C.next_ps()
                pso, pko = C.next_ps()
                C.mm(psa[0:64, 0:128], wT[:, c, :], Sb[:], True, True, ["wT", "Sb"], [pka])
                C.mm(pso[0:64, 0:128], qdec[:, cs], Sb[:], True, False, ["qdec", "Sb"], [pko])
                C.tt("dve", vnew[:], u_sb[:, c, :], psa[0:64, 0:128], OP.subtract, ["u_sb", pka], ["vnew"])
                C.mm(pso[0:64, 0:128], aqT[:, c, :], vnew[:], False, True, ["aqT", "vnew"], [pko])
                pss, pks = C.next_ps()
                C.mm(pss[:, 0:128], kdec[:, c, :], vnew[:], True, True, ["kdec", "vnew"], [pks])
                C.stt("dve", S32[:], S32[:], egl[:, c:c + 1], pss[:, 0:128], OP.mult, OP.add, ["S32", "egl", pks], ["S32"])
                C.cp("act", Sb[:], S32[:], ["S32"], ["Sb"])
                C.cp("act", o_sb[:, c, :], pso[0:64, 0:128], [pko], ["o_sb"])
            kyd = "ydo%d" % pb
            C.tt("pool", osq[:], o_sb[:], o_sb[:], OP.mult, ["o_sb"], ["osq"])
            P.op("dve", lambda e: e.tensor_reduce(out=oss[:], in_=osq[:], axis=AX.X, op=OP.add), reads=["osq"], writes=["oss"])
            C.act(oss[:], oss[:], AF.Sqrt, ["oss"], ["oss"], bias=EPS, scale=1.0 / 128.0)
            C.recip(oss[:], oss[:], ["oss"], ["oss"])
            C.tt("dve", ydo[pb][:], o_sb[:], fap(oss[:], [[1, 8], [0, 128]]), OP.mult, ["o_sb", "oss"], [kyd])
            C.tt("pool", ydo[pb][:], ydo[pb][:], fap(pv[0:64, PV_DNW:PV_DNW + 128], [[0, 8], [1, 128]]), OP.mult, [kyd, "pv"], [kyd])
            P.dma("sp", yd_d[t0:t0 + 512, :].rearrange("(c p) v -> p c v", p=64), ydo[pb][:], reads=[kyd])
        P.emit()
    return nc, P.stats


OFF_ZS, OFF_X, OFF_B, OFF_C, OFF_DT = 0, 2048, 4096, 4608, 5120
OFF_Q, OFF_K, OFF_V, OFF_ZD, OFF_BB, OFF_AA, OFF_GS, OFF_GD = 5152, 6176, 7200, 8224, 9248, 9256, 9264, 10288


def prep_B(inp, l, c):
    g = c // 2
    w = inp["w_in"][l]
    cols = np.concatenate([
        np.arange(OFF_X + 256 * c, OFF_X + 256 * c + 256),
        np.arange(OFF_B + 128 * g, OFF_B + 128 * g + 128),
        np.arange(OFF_C + 128 * g, OFF_C + 128 * g + 128),
        np.arange(OFF_Q + 128 * c, OFF_Q + 128 * c + 128),
        np.arange(OFF_K + 128 * c, OFF_K + 128 * c + 128),
        np.arange(OFF_V + 128 * c, OFF_V + 128 * c + 128),
        np.arange(OFF_DT + 4 * c, OFF_DT + 4 * c + 4),
        np.array([OFF_BB + c, OFF_AA + c]),
    ])
    wB = np.ascontiguousarray(w[:, cols])
    pv = np.zeros((128, PV_N), np.float32)
    scw, dcw = inp["ssm_conv_w"][l], inp["dn_conv_w"][l]
    chans = [("s", 256 * c), ("s", 256 * c + 128), ("s", 2048 + 128 * g), ("s", 2560 + 128 * g),
             ("d", 128 * c), ("d", 1024 + 128 * c), ("d", 2048 + 128 * c)]
    for ch, (kind, o) in enumerate(chans):
        cw = scw if kind == "s" else dcw
        pv[:, PV_CW + ch * 4:PV_CW + ch * 4 + 4] = cw[:, o:o + 128].T
        if kind == "s":
            pv[:, PV_CB + ch] = inp["ssm_conv_b"][l][o:o + 128]
    pv[:, PV_DTB:PV_DTB + 4] = inp["ssm_dt_bias"][l][4 * c:4 * c + 4][None, :]
    pv[:, PV_ALOG:PV_ALOG + 4] = inp["ssm_a_log"][l][4 * c:4 * c + 4][None, :]
    pv[:, PV_D:PV_D + 4] = inp["ssm_d"][l][4 * c:4 * c + 4][None, :]
    pv[:, PV_DDTB] = inp["dn_dt_bias"][l][c]
    pv[:, PV_DALOG] = inp["dn_a_log"][l][c]
    pv[:, PV_DNW:PV_DNW + 128] = inp["dn_norm"][l][None, :]
    return wB, pv


NV_MIXP, NV_SSMN, NV_F2, NV_F1, NV_MIX, NV_FIN, NV_N = 0, 8, 24, 32, 40, 48, 56
TT = 512
NSLOT = 6


def build_T(tail, head, final, ntok=TPC):
    ntt = ntok // TT
    nc = bass.Bass("TRN2", target_bir_lowering=False)
    dr = lambda name, shape, dt=F32: nc.dram_tensor(name, shape, dt, kind="ExternalInput").ap()
    hT_d = dr("hT", [1024, ntok])
    nv_d = dr("nv", [128, NV_N])
    cst_d = dr("cst", [128, 5, 128])
    if tail:
        ysT_d = dr("ysT", [2048, ntok])
        ydT_d = dr("ydT", [1024, ntok])
        wzg_d = dr("wzg", [1024, 5120])
        wsb_d = dr("wsb", [2048, 1024])
        wdb_d = dr("wdb", [1024, 1024])
        wout_d = dr("wout", [1024, 1024])
        f2i_d = dr("f2i", [1024, 2 * D_FF])
        f2o_d = dr("f2o", [D_FF, 1024])
    if head:
        f1i_d = dr("f1i", [1024, 2 * D_FF])
        f1o_d = dr("f1o", [D_FF, 1024])
        uT_d = nc.dram_tensor("uTo", [1024, ntok], BF16, kind="ExternalOutput").ap()
    ho_d = nc.dram_tensor("ho", [1024, ntok], F32, kind="ExternalOutput").ap()
    with contextlib.ExitStack() as st:
        C = Ctx(nc, st)
        P = C.P
        sb = C.sb
        nv = sb("nv_sb", [128, NV_N], F32)
        onesb = sb("onesb", [128, 128], BF16)
        hT = sb("hT_sb", [128, 8, TT], F32)
        xnT = sb("xnT", [128, 8, TT], BF16)
        big = sb("big", [128, 24, TT], BF16)
        mrg = sb("mrg", [128, 8, TT], BF16)
        sqb = sb("sqb", [128, 8, TT], BF16)
        ygtmp = sb("ygtmp", [128, 4, TT], F32)
        rstd = sb("rstd", [128, TT], F32)
        tmpA = [sb("tmpA%d" % i, [128, TT], F32) for i in range(2)]
        tmpB = [sb("tmpB%d" % i, [128, TT], F32) for i in range(2)]
        stg = [sb("stg%d" % i, [128, TT], F32) for i in range(2)]
        o32 = sb("o32", [128, 8, TT], F32) if final else None
        slots = [sb("slot%d" % i, [128, 6144], BF16) for i in range(NSLOT)]
        C.make_ps_ring(8)
        cnt = dict(slot=0, tA=0, tB=0, stg=0)

        P.dma("sp", nv[:], nv_d, writes=["nv"])
        P.dma("pool", onesb[:], cst_d[:, 4, :], writes=["onesb"])

        def wload(parts):
            i = cnt["slot"] % NSLOT
            cnt["slot"] += 1
            key = "slot%d" % i
            views = []
            off = 0
            for (d, col0, ncols, kc) in parts:
                v = slots[i][:, off:off + kc * ncols].rearrange("p (k n) -> p k n", n=ncols)
                P.dma("pool", v, d[:, col0:col0 + ncols].rearrange("(k p) n -> p k n", p=128), writes=[key])
                views.append(v)
                off += kc * ncols
            return views, key

        def rmsnorm(gcol, out_tile, okey, width=1024.0):
            for k in range(8):
                C.tt("pool", sqb[:, k, :], hT[:, k, :], hT[:, k, :], OP.mult, ["hT"], ["sqb"])
            pst, pk = C.next_ps()
            for k in range(8):
                C.mm(pst[:], onesb[:], sqb[:, k, :], k == 0, k == 7, ["onesb", "sqb"], [pk])
            C.act(rstd[:], pst[:], AF.Sqrt, [pk], ["rstd"], bias=EPS, scale=1.0 / width)
            C.recip(rstd[:], rstd[:], ["rstd"], ["rstd"])
            for k in range(8):
                C.stt("dve", out_tile[:, k, :], hT[:, k, :], nv[:, gcol + k:gcol + k + 1], rstd[:], OP.mult, OP.mult,
                      ["hT", "nv", "rstd"], [okey])

        def ffn(wi_d, wo_d):
            for jp in range(11):
                (vg, vu), key = wload([(wi_d, jp * 256, 256, 8), (wi_d, D_FF + jp * 256, 256, 8)])
                for jj in range(2):
                    j = 2 * jp + jj
                    cs = slice(jj * 128, (jj + 1) * 128)
                    psg, pkg = C.next_ps()
                    for k in range(8):
                        C.mm(psg[:], vg[:, k, cs], xnT[:, k, :], k == 0, k == 7, [key, "xnT"], [pkg])
                    psu, pku = C.next_ps()
                    for k in range(8):
                        C.mm(psu[:], vu[:, k, cs], xnT[:, k, :], k == 0, k == 7, [key, "xnT"], [pku])
                    ta = cnt["tA"] % 2
                    cnt["tA"] += 1
                    C.act(tmpA[ta][:], psg[:], AF.Silu, [pkg], ["tmpA%d" % ta])
                    C.tt("dve", big[:, j, :], tmpA[ta][:], psu[:], OP.mult, ["tmpA%d" % ta, pku], ["big"])
            for mp in range(4):
                (vo,), key = wload([(wo_d, mp * 256, 256, 22)])
                for mm_ in range(2):
                    m = 2 * mp + mm_
                    cs = slice(mm_ * 128, (mm_ + 1) * 128)
                    pso, pko = C.next_ps()
                    for j in range(22):
                        C.mm(pso[:], vo[:, j, cs], big[:, j, :], j == 0, j == 21, [key, "big"], [pko])
                    C.stt("dve", hT[:, m, :], pso[:], 0.5, hT[:, m, :], OP.mult, OP.add, [pko, "hT"], ["hT"])

        def stage_load(src_d, c, tsl):
            i = cnt["stg"] % 2
            cnt["stg"] += 1
            P.dma("sp", stg[i][:], src_d[c * 128:(c + 1) * 128, tsl], writes=["stg%d" % i])
            return stg[i], "stg%d" % i

        def tail_part(tsl):
            rmsnorm(NV_MIXP, xnT, "xnT")
            for g in range(4):
                (v0,), k0 = wload([(wzg_d, g * 512, 256, 8)])
                (v1,), k1 = wload([(wzg_d, g * 512 + 256, 256, 8)])
                for cc in range(4):
                    c = 4 * g + cc
                    vv, kk = (v0, k0) if cc < 2 else (v1, k1)
                    cs = slice((cc % 2) * 128, (cc % 2 + 1) * 128)
                    psz, pkz = C.next_ps()
                    for k in range(8):
                        C.mm(psz[:], vv[:, k, cs], xnT[:, k, :], k == 0, k == 7, [kk, "xnT"], [pkz])
                    ta = cnt["tA"] % 2
                    cnt["tA"] += 1
                    C.act(tmpA[ta][:], psz[:], AF.Silu, [pkz], ["tmpA%d" % ta])
                    sg, sk = stage_load(ysT_d, c, tsl)
                    C.tt("pool", ygtmp[:, cc, :], tmpA[ta][:], sg[:], OP.mult, ["tmpA%d" % ta, sk], ["ygtmp"])
                    C.tt("pool", sqb[:, cc, :], ygtmp[:, cc, :], ygtmp[:, cc, :], OP.mult, ["ygtmp"], ["sqb"])
                psn, pkn = C.next_ps()
                for cc in range(4):
                    C.mm(psn[:], onesb[:], sqb[:, cc, :], cc == 0, cc == 3, ["onesb", "sqb"], [pkn])
                C.act(rstd[:], psn[:], AF.Sqrt, [pkn], ["rstd"], bias=EPS, scale=1.0 / 512.0)
                C.recip(rstd[:], rstd[:], ["rstd"], ["rstd"])
                for cc in range(4):
                    c = 4 * g + cc
                    C.stt("dve", big[:, c, :], ygtmp[:, cc, :], nv[:, NV_SSMN + c:NV_SSMN + c + 1], rstd[:], OP.mult, OP.mult,
                          ["ygtmp", "nv", "rstd"], ["big"])
            for cp in range(4):
                (vz,), kz = wload([(wzg_d, 2048 + cp * 256, 256, 8)])
                for cc in range(2):
                    c = 2 * cp + cc
                    cs = slice(cc * 128, (cc + 1) * 128)
                    psz, pkz = C.next_ps()
                    for k in range(8):
                        C.mm(psz[:], vz[:, k, cs], xnT[:, k, :], k == 0, k == 7, [kz, "xnT"], [pkz])
                    ta = cnt["tA"] % 2
                    cnt["tA"] += 1
                    C.act(tmpA[ta][:], psz[:], AF.Silu, [pkz], ["tmpA%d" % ta])
                    sg, sk = stage_load(ydT_d, c, tsl)
                    C.tt("pool", big[:, 16 + c, :], tmpA[ta][:], sg[:], OP.mult, ["tmpA%d" % ta, sk], ["big"])
            for mp in range(4):
                (vS,), kS = wload([(wsb_d, mp * 256, 256, 16)])
                (vD, vGs, vGd), kD = wload([(wdb_d, mp * 256, 256, 8), (wzg_d, 3072 + mp * 256, 256, 8),
                                            (wzg_d, 4096 + mp * 256, 256, 8)])
                for mm_ in range(2):
                    m = 2 * mp + mm_
                    cs = slice(mm_ * 128, (mm_ + 1) * 128)
                    ps1, pk1 = C.next_ps()
                    for c in range(16):
                        C.mm(ps1[:], vS[:, c, cs], big[:, c, :], c == 0, c == 15, [kS, "big"], [pk1])
                    ps2, pk2 = C.next_ps()
                    for k in range(8):
                        C.mm(ps2[:], vGs[:, k, cs], xnT[:, k, :], k == 0, k == 7, [kD, "xnT"], [pk2])
                    ps3, pk3 = C.next_ps()
                    for c in range(8):
                        C.mm(ps3[:], vD[:, c, cs], big[:, 16 + c, :], c == 0, c == 7, [kD, "big"], [pk3])
                    ps4, pk4 = C.next_ps()
                    for k in range(8):
                        C.mm(ps4[:], vGd[:, k, cs], xnT[:, k, :], k == 0, k == 7, [kD, "xnT"], [pk4])
                    ta = cnt["tA"] % 2
                    cnt["tA"] += 1
                    tb = cnt["tB"] % 2
                    cnt["tB"] += 1
                    kta, ktb = "tmpA%d" % ta, "tmpB%d" % tb
                    C.act(tmpA[ta][:], ps2[:], AF.Sigmoid, [pk2], [kta])
                    C.tt("dve", tmpA[ta][:], tmpA[ta][:], ps1[:], OP.mult, [kta, pk1], [kta])
                    C.act(tmpB[tb][:], ps4[:], AF.Sigmoid, [pk4], [ktb])
                    C.tt("dve", tmpB[tb][:], tmpB[tb][:], ps3[:], OP.mult, [ktb, pk3], [ktb])
                    C.tt("pool", mrg[:, m, :], tmpA[ta][:], tmpB[tb][:], OP.add, [kta, ktb], ["mrg"])
            for mp in range(4):
                (vo,), ko = wload([(wout_d, mp * 256, 256, 8)])
                for mm_ in range(2):
                    m = 2 * mp + mm_
                    cs = slice(mm_ * 128, (mm_ + 1) * 128)
                    pso, pko = C.next_ps()
                    for k in range(8):
                        C.mm(pso[:], vo[:, k, cs], mrg[:, k, :], k == 0, k == 7, [ko, "mrg"], [pko])
                    C.tt("dve", hT[:, m, :], hT[:, m, :], pso[:], OP.add, ["hT", pko], ["hT"])

        for t in range(ntt):
            tsl = slice(t * TT, (t + 1) * TT)
            P.dma("sp", hT[:], hT_d[:, tsl].rearrange("(k p) t -> p k t", p=128), writes=["hT"])
            if tail:
                tail_part(tsl)
                rmsnorm(NV_F2, xnT, "xnT")
                ffn(f2i_d, f2o_d)
            if head:
                rmsnorm(NV_F1, xnT, "xnT")
                ffn(f1i_d, f1o_d)
                rmsnorm(NV_MIX, xnT, "xnT")
                P.dma("sp", uT_d[:, tsl].rearrange("(k p) t -> p k t", p=128), xnT[:], reads=["xnT"])
            if final:
                rmsnorm(NV_FIN, o32, "o32")
                P.dma("sp", ho_d[:, tsl].rearrange("(k p) t -> p k t", p=128), o32[:], reads=["o32"])
            else:
                P.dma("sp", ho_d[:, tsl].rearrange("(k p) t -> p k t", p=128), hT[:], reads=["hT"])
        P.emit()
    return nc, P.stats


def nv_pack(inp, l_prev, l_cur):
    nv = np.zeros((128, NV_N), np.float32)
    col = lambda v: np.ascontiguousarray(v.reshape(-1, 128).T)
    if l_prev is not None:
        nv[:, NV_MIXP:NV_MIXP + 8] = col(inp["mix_norm"][l_prev])
        nv[:, NV_SSMN:NV_SSMN + 16] = col(inp["ssm_norm"][l_prev])
        nv[:, NV_F2:NV_F2 + 8] = col(inp["ffn2_norm"][l_prev])
    if l_cur is not None:
        nv[:, NV_F1:NV_F1 + 8] = col(inp["ffn1_norm"][l_cur])
        nv[:, NV_MIX:NV_MIX + 8] = col(inp["mix_norm"][l_cur])
    nv[:, NV_FIN:NV_FIN + 8] = col(inp["final_norm"])
    return nv


def wzg_cols(inp, l):
    w = inp["w_in"][l]
    return np.ascontiguousarray(np.concatenate(
        [w[:, OFF_ZS:OFF_ZS + 2048], w[:, OFF_ZD:OFF_ZD + 1024], w[:, OFF_GS:OFF_GS + 1024], w[:, OFF_GD:OFF_GD + 1024]], axis=1))


_PROGS = {}


def _prog(kind, *a):
    key = (kind,) + a
    if key not in _PROGS:
        _PROGS[key] = (build_B(*a) if kind == "B" else build_T(*a))[0]
    return _PROGS[key]


def kernel(**inputs):
    inp = {k: np.asarray(v) for k, v in inputs.items()}
    cst = _consts()
    cores = list(range(NCORE))
    x = inp["x"][0]
    hT = [np.ascontiguousarray(x[c * TPC:(c + 1) * TPC].T) for c in cores]
    ys_all = yd_all = None
    out = None
    for l in range(DEPTH + 1):
        tail, head, final = l > 0, l < DEPTH, l == DEPTH
        nc = _prog("T", tail, head, final)
        nv = nv_pack(inp, l - 1 if tail else None, l if head else None)
        maps = []
        for c in cores:
            m = {"hT": hT[c], "nv": nv, "cst": cst}
            if tail:
                tk = slice(c * TPC, (c + 1) * TPC)
                m.update(ysT=np.ascontiguousarray(ys_all[tk].T), ydT=np.ascontiguousarray(yd_all[tk].T),
                         wzg=wzg_cols(inp, l - 1), wsb=inp["ssm_w_branch"][l - 1], wdb=inp["dn_w_branch"][l - 1],
                         wout=inp["w_out"][l - 1], f2i=inp["ffn2_w_in"][l - 1], f2o=inp["ffn2_w_out"][l - 1])
            if head:
                m.update(f1i=inp["ffn1_w_in"][l], f1o=inp["ffn1_w_out"][l])
            maps.append(m)
        res = run_bass_kernel_spmd(nc, maps, core_ids=cores).results
        if final:
            out = np.concatenate([res[c]["ho"].T for c in cores], axis=0)[None]
            break
        hT = [res[c]["ho"] for c in cores]
        uT_all = np.ascontiguousarray(np.concatenate([res[c]["uTo"] for c in cores], axis=1))
        ncb = _prog("B", SEQ)
        maps = []
        for c in cores:
            wB, pv = prep_B(inp, l, c)
            maps.append({"uT": uT_all, "wB": wB, "cst": cst, "pv": pv})
        res = run_bass_kernel_spmd(ncb, maps, core_ids=cores).results
        ys_all = np.concatenate([res[c]["ys"] for c in cores], axis=1)
        yd_all = np.concatenate([res[c]["yd"] for c in cores], axis=1)
    return np.ascontiguousarray(out.astype(np.float32))
```
